# Optimizing a Trainium2 kernel written in Bass

```python
import jax, jax.numpy as jnp
from jax import lax
import numpy as np

D_MODEL = 2048
BATCH = 16
SEQ = 2048
DEPTH = 2

D_CONV = D_MODEL // 2
CONV_WIDTH = 31
D_POOL = D_MODEL // 2
POOL_WINDOWS = (2, 4, 8, 16)
POOL_GROUP = D_POOL // len(POOL_WINDOWS)
D_SHORT = D_MODEL
SHORT_WIDTH = 3
D_FF = -(-8 * D_MODEL // (3 * 256)) * 256
N_EVEN = (DEPTH + 1) // 2
N_ODD = DEPTH // 2
EPS = 1e-6

kernel_name = 'hybrid_conformer_pool_shortconv_block'


def rms_norm(x, g):
    xf = x.astype(jnp.float32)
    y = xf * lax.rsqrt(jnp.mean(xf * xf, axis=-1, keepdims=True) + EPS)
    return (y * g.astype(jnp.float32)).astype(x.dtype)


def layer_norm(x, g, b):
    xf = x.astype(jnp.float32)
    mu = jnp.mean(xf, axis=-1, keepdims=True)
    xc = xf - mu
    var = jnp.mean(xc * xc, axis=-1, keepdims=True)
    y = xc * lax.rsqrt(var + EPS) * g.astype(jnp.float32) + b.astype(jnp.float32)
    return y.astype(x.dtype)


def causal_depthwise_conv(x, w):
    k, c = w.shape
    return lax.conv_general_dilated(
        x, w[:, None, :].astype(x.dtype), window_strides=(1,),
        padding=[(k - 1, 0)], dimension_numbers=('NWC', 'WIO', 'NWC'),
        feature_group_count=c)


def multiscale_pool(v, w_pool, scale):
    b, s, _ = v.shape
    vf = v.astype(jnp.float32)
    cnt_pos = jnp.arange(s) + 1
    outs = []
    for g, w in enumerate(POOL_WINDOWS):
        xg = vf[..., g * POOL_GROUP:(g + 1) * POOL_GROUP]
        cs = jnp.cumsum(xg, axis=1)
        lag = jnp.pad(cs, ((0, 0), (w, 0), (0, 0)))[:, :s]
        cnt = jnp.minimum(cnt_pos, w).astype(jnp.float32)[None, :, None]
        outs.append((cs - lag) / cnt - xg)
    p = jnp.stack(outs, axis=2).astype(v.dtype)
    p = jnp.einsum('bsgc,gcd->bsgd', p, w_pool).reshape(b, s, D_POOL)
    return p * scale


def conv_pool_mixer(x, norm_g, w_in, conv_w, conv_b, ln_g, ln_b, w_pool, pool_scale, w_out):
    h = rms_norm(x, norm_g)
    u = h @ w_in
    a_val = u[..., :D_CONV]
    a_gate = u[..., D_CONV:2 * D_CONV]
    b_in = u[..., 2 * D_CONV:]
    a = a_val * jax.nn.sigmoid(a_gate)
    a = causal_depthwise_conv(a, conv_w) + conv_b
    a = jax.nn.silu(layer_norm(a, ln_g, ln_b))
    p = multiscale_pool(b_in, w_pool, pool_scale)
    return jnp.concatenate([a, p], axis=-1) @ w_out


def short_conv_mixer(x, norm_g, w_in, conv_w, w_out):
    h = rms_norm(x, norm_g)
    u = h @ w_in
    gate_b = u[..., :D_SHORT]
    gate_c = u[..., D_SHORT:2 * D_SHORT]
    v = u[..., 2 * D_SHORT:]
    y = gate_b * causal_depthwise_conv(gate_c * v, conv_w)
    return y @ w_out


def swiglu(h, w_gate, w_up, w_down):
    return (jax.nn.silu(h @ w_gate) * (h @ w_up)) @ w_down


def _normal(k, shape, fan_in):
    return jax.random.normal(k, shape, jnp.float32) * (fan_in ** -0.5)


def setup_inputs(seed: int = 0) -> dict:
    key = jax.random.key(seed)
    ks = jax.random.split(key, 20)
    d = D_MODEL
    x = jax.random.normal(ks[0], (BATCH, SEQ, d), jnp.float32)
    mix_norm_e = 1.0 + 0.02 * jax.random.normal(ks[1], (N_EVEN, d), jnp.float32)
    w_in_e = _normal(ks[2], (N_EVEN, d, 2 * D_CONV + D_POOL), d)
    conv_w_e = _normal(ks[3], (N_EVEN, CONV_WIDTH, D_CONV), CONV_WIDTH)
    conv_b_e = 0.02 * jax.random.normal(ks[4], (N_EVEN, D_CONV), jnp.float32)
    ln_g_e = 1.0 + 0.02 * jax.random.normal(ks[5], (N_EVEN, D_CONV), jnp.float32)
    ln_b_e = 0.02 * jax.random.normal(ks[6], (N_EVEN, D_CONV), jnp.float32)
    w_pool_e = _normal(ks[7], (N_EVEN, len(POOL_WINDOWS), POOL_GROUP, POOL_GROUP), POOL_GROUP)
    pool_scale_e = 1.0 + 0.02 * jax.random.normal(ks[8], (N_EVEN, D_POOL), jnp.float32)
    w_out_e = _normal(ks[9], (N_EVEN, D_CONV + D_POOL, d), D_CONV + D_POOL)
    mix_norm_o = 1.0 + 0.02 * jax.random.normal(ks[10], (N_ODD, d), jnp.float32)
    w_in_o = _normal(ks[11], (N_ODD, d, 3 * D_SHORT), d)
    conv_w_o = _normal(ks[12], (N_ODD, SHORT_WIDTH, D_SHORT), SHORT_WIDTH)
    w_out_o = _normal(ks[13], (N_ODD, D_SHORT, d), D_SHORT)
    ffn_norm = 1.0 + 0.02 * jax.random.normal(ks[14], (DEPTH, d), jnp.float32)
    w_gate = _normal(ks[15], (DEPTH, d, D_FF), d)
    w_up = _normal(ks[16], (DEPTH, d, D_FF), d)
    w_down = _normal(ks[17], (DEPTH, D_FF, d), D_FF)
    final_norm = 1.0 + 0.02 * jax.random.normal(ks[18], (d,), jnp.float32)
    return {'x': x, 'mix_norm_e': mix_norm_e, 'w_in_e': w_in_e, 'conv_w_e': conv_w_e,
            'conv_b_e': conv_b_e, 'ln_g_e': ln_g_e, 'ln_b_e': ln_b_e, 'w_pool_e': w_pool_e,
            'pool_scale_e': pool_scale_e, 'w_out_e': w_out_e, 'mix_norm_o': mix_norm_o,
            'w_in_o': w_in_o, 'conv_w_o': conv_w_o, 'w_out_o': w_out_o, 'ffn_norm': ffn_norm,
            'w_gate': w_gate, 'w_up': w_up, 'w_down': w_down, 'final_norm': final_norm}


def reference(x, mix_norm_e, w_in_e, conv_w_e, conv_b_e, ln_g_e, ln_b_e, w_pool_e,
              pool_scale_e, w_out_e, mix_norm_o, w_in_o, conv_w_o, w_out_o, ffn_norm,
              w_gate, w_up, w_down, final_norm):
    h = x
    for i in range(DEPTH):
        j = i // 2
        if i % 2 == 0:
            h = h + conv_pool_mixer(h, mix_norm_e[j], w_in_e[j], conv_w_e[j], conv_b_e[j],
                                    ln_g_e[j], ln_b_e[j], w_pool_e[j], pool_scale_e[j],
                                    w_out_e[j])
        else:
            h = h + short_conv_mixer(h, mix_norm_o[j], w_in_o[j], conv_w_o[j], w_out_o[j])
        h = h + swiglu(rms_norm(h, ffn_norm[i]), w_gate[i], w_up[i], w_down[i])
    return rms_norm(h, final_norm)
```

```python
from contextlib import ExitStack

import numpy as np
import concourse.bass as bass
import concourse.mybir as mybir
from concourse.bass_utils import run_bass_kernel_spmd

F32 = mybir.dt.float32
BF16 = mybir.dt.bfloat16
AF = mybir.ActivationFunctionType
ALU = mybir.AluOpType

P = 128
D = 2048
KC = 16
SEQ = 2048
DFF = 5632
NJ = 44
NG = 4
GJ = 11
NS = 2
TS = [416, 416, 416, 400, 400]
TMAX = 416
HA = 30
HP = 16
EPS = 1e-6
NCORES = 8

C_MIXE = 0
C_FFN0 = 16
C_MIXO = 32
C_FFN1 = 48
C_FIN = 64
C_CONVW = 80
C_CONVB = C_CONVW + 248
C_LNG = C_CONVB + 8
C_LNB = C_LNG + 8
C_PSC = C_LNB + 8
C_CW3 = C_PSC + 8
C_INVC = C_CW3 + 48
C_EPS = C_INVC + 64
C_ID = C_EPS + 1
NCST = C_ID + 128


class _Task:
    __slots__ = ("eng", "fn", "deps", "is_dma", "ndma", "chan", "sig", "needs_sig", "idx")


class Sched:
    def __init__(self):
        self.tasks = []
        self.last_writer = {}
        self.readers = {}
        self.chan_last = {}

    def add(self, eng, fn, reads=(), writes=(), chan=None, ndma=1):
        t = _Task()
        t.eng = eng
        t.fn = fn
        t.is_dma = chan is not None
        t.ndma = ndma
        t.chan = chan
        t.sig = None
        t.needs_sig = chan is not None
        t.idx = len(self.tasks)
        deps = {}

        def dep(o, raw):
            if o is None or o is t:
                return
            if (not o.is_dma) and (not t.is_dma) and o.eng == eng:
                if eng == "pe" or not raw:
                    return
            deps[o.idx] = o

        for r in reads:
            dep(self.last_writer.get(r), True)
        for w in writes:
            dep(self.last_writer.get(w), False)
            for rd in self.readers.get(w, ()):
                dep(rd, False)
        if chan is not None:
            dep(self.chan_last.get(chan), True)
            self.chan_last[chan] = t
        for r in reads:
            self.readers.setdefault(r, []).append(t)
        for w in writes:
            self.last_writer[w] = t
            self.readers[w] = []
        t.deps = list(deps.values())
        for o in t.deps:
            o.needs_sig = True
        self.tasks.append(t)
        return t

    def finalize(self, eng_sems, chan_sems, epoch_of):
        cnt = {}
        for t in self.tasks:
            if not t.needs_sig:
                continue
            if t.is_dma:
                s = chan_sems[t.chan]
                cnt[s.name] = cnt.get(s.name, 0) + 16 * t.ndma
                t.sig = (s, cnt[s.name])
            else:
                s = eng_sems[t.eng][epoch_of(t)]
                cnt[s.name] = cnt.get(s.name, 0) + 1
                t.sig = (s, cnt[s.name])

    def emit(self, eng, e):
        waited = {}
        for t in self.tasks:
            if t.eng != eng:
                continue
            for o in sorted(t.deps, key=lambda o: o.idx):
                s, v = o.sig
                if waited.get(s.name, 0) < v:
                    e.wait_ge(s, v)
                    waited[s.name] = v
            res = t.fn(e)
            if t.needs_sig:
                s, v = t.sig
                if t.is_dma:
                    assert isinstance(res, list) and len(res) == t.ndma
                    for ins in res:
                        ins.then_inc(s, 16)
                else:
                    res.then_inc(s, 1)


class _Rot:
    def __init__(self, items):
        self.items = list(items)
        self.i = 0

    def next(self):
        v = self.items[self.i % len(self.items)]
        self.i += 1
        return v


def build_program(nrounds=5, layers=("l0", "f0", "l1", "f1"), final_norm=True):
    nc = bass.Bass("TRN2", target_bir_lowering=False)
    dt = lambda name, shape: nc.dram_tensor(name, shape, F32, kind="ExternalInput").ap()
    x_d = dt("x", [NS, P, KC, SEQ])
    cst_d = dt("cst", [P, NCST])
    wine_d = dt("wine", [12, P, 4096])
    wpool_d = dt("wpool", [1, P, 2048])
    woute_d = dt("woute", [8, P, 4096])
    wgu_d = dt("wgu", [2 * NJ, P, 4096])
    wdn_d = dt("wdn", [2 * NG * 8, P, GJ * 256])
    wino_cv_d = dt("wino_cv", [16, P, 4096])
    wino_b_d = dt("wino_b", [16, P, 2048])
    wouto_d = dt("wouto", [8, P, 4096])
    out_d = nc.dram_tensor("out", [NS, P, KC, SEQ], F32, kind="ExternalOutput").ap()

    sch = Sched()
    cur_round = [0]
    task_round = {}

    def add(eng, fn, reads=(), writes=(), chan=None, ndma=1):
        t = sch.add(eng, fn, reads, writes, chan, ndma)
        task_round[t.idx] = cur_round[0]
        return t

    with ExitStack() as es:
        sb = lambda name, shape, dty: es.enter_context(nc.sbuf_tensor(name, shape, dty))
        h = [sb(f"h{s}", [P, KC, TMAX], F32) for s in range(NS)]
        xn = [sb(f"xn{s}", [P, KC, TMAX], BF16) for s in range(NS)]
        mix = [sb(f"mix{s}", [P, KC, TMAX], BF16) for s in range(NS)]
        abf = [sb(f"abf{s}", [P, 8, 448], BF16) for s in range(NS)]
        diag = [sb(f"diag{i}", [P, 31, P], BF16) for i in range(2)]
        ring = [sb(f"ring{i}", [P, 4096], BF16) for i in range(4)]
        cst = sb("cst_s", [P, NCST], F32)
        ones = sb("ones", [P, P], BF16)
        rstd = [sb(f"rstd{s}", [P, TMAX], F32) for s in range(NS)]
        lnM = [sb(f"lnM{s}", [P, TMAX], F32) for s in range(NS)]
        lnR = [sb(f"lnR{s}", [P, TMAX], F32) for s in range(NS)]
        lnB = [sb(f"lnB{s}", [P, TMAX], F32) for s in range(NS)]
        carryP = [sb(f"carryP{s}", [P, 8, HP], F32) for s in range(NS)]
        carry1 = [sb(f"carry1{s}", [P, KC, 2], F32) for s in range(NS)]
        fix = sb("fix", [P, 16], F32)
        NTF, NTH = 10, 6
        tmpF = [sb(f"tmpF{i}", [P, HP + TMAX], F32) for i in range(NTF)]
        tmpH = [sb(f"tmpH{i}", [P, TMAX], BF16) for i in range(NTH)]
        ps = [es.enter_context(nc.psum_tensor(f"ps{i}", [P, 512], F32)) for i in range(8)]

        rF = _Rot(range(NTF))
        rH = _Rot(range(NTH))
        mm6 = _Rot(range(6))
        mm4 = _Rot(range(4))
        ochan = _Rot(range(8))

        cc = lambda col: cst[:, col:col + 1]

        def c_view(s, j, T):
            return xn[s][:, 2 * j:2 * j + 2, :].rearrange("p a t -> p (a t)").bitcast(F32)[:, :T]

        blocks = []
        if "l0" in layers:
            blocks += [(f"wine_a{j}", wine_d[j], 4096) for j in range(8)]
            blocks += [(f"wine_b{i}", wine_d[8 + i], 4096) for i in range(4)]
            blocks += [("wpool", wpool_d[0], 2048)]
            blocks += [(f"woute{m}", woute_d[m], 4096) for m in range(8)]

        def ffn_blocks(l):
            out = []
            for g in range(NG):
                out += [(f"wgu{l}_{g * GJ + jj}", wgu_d[l * NJ + g * GJ + jj], 4096) for jj in range(GJ)]
                out += [(f"wdn{l}_{g}_{m}", wdn_d[(l * NG + g) * 8 + m], GJ * 256) for m in range(8)]
            return out

        if "f0" in layers:
            blocks += ffn_blocks(0)
        if "l1" in layers:
            for j in range(16):
                blocks += [(f"wino_cv{j}", wino_cv_d[j], 4096), (f"wino_b{j}", wino_b_d[j], 2048)]
            blocks += [(f"wouto{m}", wouto_d[m], 4096) for m in range(8)]
        if "f1" in layers:
            blocks += ffn_blocks(1)
        wseq = blocks * nrounds
        wstate = {"issued": 0, "consumed": 0}
        NSLOT, AHEAD = 4, 2

        def w_issue(i):
            name, src, nel = wseq[i]
            slot = i % NSLOT
            add("pool", lambda e, slot=slot, src=src, nel=nel: [e.dma_start(out=ring[slot][:, :nel], in_=src)],
                writes=[("ring", slot)], chan=f"w{slot}")

        def w_get(name):
            i = wstate["consumed"]
            assert wseq[i][0] == name, (wseq[i][0], name)
            while wstate["issued"] < min(len(wseq), i + AHEAD + 1):
                w_issue(wstate["issued"])
                wstate["issued"] += 1
            wstate["consumed"] += 1
            return i % NSLOT

        def pe_group(bank, T, pairs, reads, start=True, stop=True, fine=None):
            if fine is not None:
                n = len(pairs)
                for i, (l, r) in enumerate(pairs):
                    add("pe", lambda e, l=l, r=r, i=i, n=n: e.matmul(ps[bank][:, :T], l, r, start=(start and i == 0),
                                                                    stop=(stop and i == n - 1)),
                        reads=fine[i], writes=[("ps", bank)])
                return

            def fn(e, bank=bank, T=T, pairs=pairs, start=start, stop=stop):
                n = len(pairs)
                ins = None
                for i, (l, r) in enumerate(pairs):
                    ins = e.matmul(ps[bank][:, :T], l, r, start=(start and i == 0), stop=(stop and i == n - 1))
                return ins
            add("pe", fn, reads=reads, writes=[("ps", bank)])

        def rmsnorm(s, gcol, T, t0, final=False):
            bank = 6 + s
            for kc in range(KC):
                th = rH.next()
                add("act", lambda e, kc=kc, th=th: e.activation(out=tmpH[th][:, :T], in_=h[s][:, kc, :T], func=AF.Square),
                    reads=[("h", s, kc)], writes=[("tmpH", th)])
                pe_group(bank, T, [(ones[:], tmpH[th][:, :T])], [("tmpH", th), ("ones",)],
                         start=(kc == 0), stop=(kc == KC - 1))
            tf = rF.next()
            add("act", lambda e, tf=tf: e.activation(out=tmpF[tf][:, :T], in_=ps[bank][:, :T], func=AF.Sqrt,
                                                     bias=cc(C_EPS), scale=1.0 / D),
                reads=[("ps", bank), ("cst",)], writes=[("tmpF", tf)])
            add("dve", lambda e, tf=tf: e.reciprocal(out=rstd[s][:, :T], in_=tmpF[tf][:, :T]),
                reads=[("tmpF", tf)], writes=[("rstd", s)])
            for kc in range(KC):
                if not final:
                    add("dve", lambda e, kc=kc: e.scalar_tensor_tensor(
                        out=xn[s][:, kc, :T], in0=h[s][:, kc, :T], scalar=cc(gcol + kc), in1=rstd[s][:, :T],
                        op0=ALU.mult, op1=ALU.mult),
                        reads=[("h", s, kc), ("rstd", s), ("cst",)], writes=[("xn", s, kc)])
                else:
                    tf = rF.next()
                    add("dve", lambda e, kc=kc, tf=tf: e.scalar_tensor_tensor(
                        out=tmpF[tf][:, :T], in0=h[s][:, kc, :T], scalar=cc(gcol + kc), in1=rstd[s][:, :T],
                        op0=ALU.mult, op1=ALU.mult),
                        reads=[("h", s, kc), ("rstd", s), ("cst",)], writes=[("tmpF", tf)])
                    oc = ochan.next()
                    add("sp", lambda e, kc=kc, tf=tf: [e.dma_start(out=out_d[s, :, kc, t0:t0 + T], in_=tmpF[tf][:, :T])],
                        reads=[("tmpF", tf)], writes=[("ochan", oc)], chan=f"o{oc}")

        def raw_store(s, T, t0):
            for kc in range(KC):
                oc = ochan.next()
                add("sp", lambda e, kc=kc: [e.dma_start(out=out_d[s, :, kc, t0:t0 + T], in_=h[s][:, kc, :T])],
                    reads=[("h", s, kc)], writes=[("ochan", oc)], chan=f"o{oc}")

        def resid_proj(wname_fn, nk, T, pair=False):
            def emit(mp, slot, s):
                for c in range(2):
                    m = 2 * mp + c
                    b = mm6.next()
                    pairs = [(ring[slot][:, k * 256 + c * 128:k * 256 + c * 128 + 128], mix[s][:, k, :T])
                             for k in range(nk)]
                    pe_group(b, T, pairs, [("ring", slot)] + [("mix", s, k) for k in range(nk)])
                    add("dve", lambda e, s=s, m=m, b=b: e.tensor_tensor(
                        out=h[s][:, m, :T], in0=h[s][:, m, :T], in1=ps[b][:, :T], op=ALU.add),
                        reads=[("h", s, m), ("ps", b)], writes=[("h", s, m)])

            if pair:
                for mp in range(0, 8, 2):
                    s0 = w_get(wname_fn(mp))
                    s1 = w_get(wname_fn(mp + 1))
                    for s in range(NS):
                        emit(mp, s0, s)
                        emit(mp + 1, s1, s)
            else:
                for mp in range(8):
                    slot = w_get(wname_fn(mp))
                    for s in range(NS):
                        emit(mp, slot, s)

        def l0_mixer(r, T):
            def build_diag(q):
                j, d = q % 8, q % 2

                def fn(e, j=j, d=d):
                    ins = None
                    for k in range(31):
                        ins = e.tensor_scalar(out=diag[d][:, k, :], in0=cst[:, C_ID:C_ID + P],
                                              scalar1=cc(C_CONVW + j * 31 + k), scalar2=None, op0=ALU.mult)
                    return ins
                add("pool", fn, reads=[("cst",)], writes=[("diag", d)])

            build_diag(0)
            build_diag(1)
            for s in range(NS):
                rmsnorm(s, C_MIXE, T, 0)
            xr = lambda s: [("xn", s, k) for k in range(KC)]
            for j in range(8):
                slot = w_get(f"wine_a{j}")
                for s in range(NS):
                    bg = mm6.next()
                    pe_group(bg, T, [(ring[slot][:, k * 256:k * 256 + 128], xn[s][:, k, :T]) for k in range(KC)],
                             [("ring", slot)] + xr(s),
                             fine=([[("ring", slot), ("xn", s, k)] for k in range(KC)] if j == 0 else None))
                    tf = rF.next()
                    add("act", lambda e, bg=bg, tf=tf: e.activation(out=tmpF[tf][:, :T], in_=ps[bg][:, :T], func=AF.Sigmoid),
                        reads=[("ps", bg)], writes=[("tmpF", tf)])
                    bv = mm6.next()
                    pe_group(bv, T, [(ring[slot][:, k * 256 + 128:k * 256 + 256], xn[s][:, k, :T]) for k in range(KC)],
                             [("ring", slot)] + xr(s))
                    add("dve", lambda e, s=s, j=j, bv=bv, tf=tf: e.tensor_tensor(
                        out=abf[s][:, j, HA:HA + T], in0=ps[bv][:, :T], in1=tmpF[tf][:, :T], op=ALU.mult),
                        reads=[("ps", bv), ("tmpF", tf)], writes=[("abf", s, j)])
            for i in range(4):
                slot = w_get(f"wine_b{i}")
                for s in range(NS):
                    for c in range(2):
                        j = 2 * i + c
                        g = j // 2
                        w = 2 << g
                        b = mm6.next()
                        pe_group(b, T, [(ring[slot][:, k * 256 + c * 128:k * 256 + c * 128 + 128], xn[s][:, k, :T])
                                        for k in range(KC)], [("ring", slot)] + xr(s))
                        xb = rF.next()
                        L = HP + T
                        add("act", lambda e, b=b, xb=xb: e.activation(out=tmpF[xb][:, HP:HP + T], in_=ps[b][:, :T], func=AF.Copy),
                            reads=[("ps", b)], writes=[("tmpF", xb)])
                        add("dve", lambda e, s=s, j=j, xb=xb: e.tensor_copy(out=tmpF[xb][:, 0:HP], in_=carryP[s][:, j, :]),
                            reads=[("carryP", s, j), ("tmpF", xb)], writes=[("tmpF", xb)])
                        sa = rF.next()
                        add("dve", lambda e, xb=xb, sa=sa, L=L: e.tensor_tensor(
                            out=tmpF[sa][:, 1:L], in0=tmpF[xb][:, 1:L], in1=tmpF[xb][:, 0:L - 1], op=ALU.add),
                            reads=[("tmpF", xb)], writes=[("tmpF", sa)])
                        cur = sa
                        if g >= 1:
                            sbb = rF.next()
                            add("dve", lambda e, sa=sa, sbb=sbb, L=L: e.tensor_tensor(
                                out=tmpF[sbb][:, 3:L], in0=tmpF[sa][:, 3:L], in1=tmpF[sa][:, 1:L - 2], op=ALU.add),
                                reads=[("tmpF", sa)], writes=[("tmpF", sbb)])
                            cur = sbb
                        if g >= 2:
                            add("dve", lambda e, sa=sa, sbb=sbb, L=L: e.tensor_tensor(
                                out=tmpF[sa][:, 7:L], in0=tmpF[sbb][:, 7:L], in1=tmpF[sbb][:, 3:L - 4], op=ALU.add),
                                reads=[("tmpF", sbb)], writes=[("tmpF", sa)])
                            cur = sa
                        if g >= 3:
                            add("dve", lambda e, sa=sa, sbb=sbb, L=L: e.tensor_tensor(
                                out=tmpF[sbb][:, 15:L], in0=tmpF[sa][:, 15:L], in1=tmpF[sa][:, 7:L - 8], op=ALU.add),
                                reads=[("tmpF", sa)], writes=[("tmpF", sbb)])
                            cur = sbb
                        add("dve", lambda e, s=s, j=j, cur=cur, xb=xb, w=w: e.scalar_tensor_tensor(
                            out=mix[s][:, j, :T], in0=tmpF[cur][:, HP:HP + T], scalar=1.0 / w, in1=tmpF[xb][:, HP:HP + T],
                            op0=ALU.mult, op1=ALU.subtract),
                            reads=[("tmpF", cur), ("tmpF", xb)], writes=[("mix", s, j)])
                        if r == 0:
                            add("dve", lambda e, cur=cur, g=g: e.tensor_tensor(
                                out=fix[:], in0=tmpF[cur][:, HP:HP + 16], in1=cst[:, C_INVC + 16 * g:C_INVC + 16 * g + 16],
                                op=ALU.mult),
                                reads=[("tmpF", cur), ("cst",)], writes=[("fix",)])
                            add("dve", lambda e, s=s, j=j, xb=xb: e.tensor_tensor(
                                out=mix[s][:, j, 0:16], in0=fix[:], in1=tmpF[xb][:, HP:HP + 16], op=ALU.subtract),
                                reads=[("fix",), ("tmpF", xb), ("mix", s, j)], writes=[("mix", s, j)])
                        add("dve", lambda e, s=s, j=j, xb=xb: e.tensor_copy(out=carryP[s][:, j, :], in_=tmpF[xb][:, T:T + HP]),
                            reads=[("tmpF", xb)], writes=[("carryP", s, j)])

            def ln_finalize(s):
                b1, b2 = 4 + 2 * s, 5 + 2 * s
                t1 = rF.next()
                add("dve", lambda e, s=s, b1=b1: e.tensor_scalar(out=lnM[s][:, :T], in0=ps[b1][:, :T], scalar1=1.0 / 1024,
                                                               scalar2=None, op0=ALU.mult),
                    reads=[("ps", b1)], writes=[("lnM", s)])
                add("dve", lambda e, s=s, t1=t1: e.tensor_tensor(out=tmpF[t1][:, :T], in0=lnM[s][:, :T], in1=lnM[s][:, :T],
                                                               op=ALU.mult),
                    reads=[("lnM", s)], writes=[("tmpF", t1)])
                add("dve", lambda e, s=s, t1=t1, b2=b2: e.scalar_tensor_tensor(
                    out=tmpF[t1][:, :T], in0=ps[b2][:, :T], scalar=1.0 / 1024, in1=tmpF[t1][:, :T],
                    op0=ALU.mult, op1=ALU.subtract),
                    reads=[("ps", b2), ("tmpF", t1)], writes=[("tmpF", t1)])
                add("act", lambda e, t1=t1: e.activation(out=tmpF[t1][:, :T], in_=tmpF[t1][:, :T], func=AF.Sqrt,
                                                         bias=cc(C_EPS), scale=1.0),
                    reads=[("tmpF", t1), ("cst",)], writes=[("tmpF", t1)])
                add("dve", lambda e, s=s, t1=t1: e.reciprocal(out=lnR[s][:, :T], in_=tmpF[t1][:, :T]),
                    reads=[("tmpF", t1)], writes=[("lnR", s)])
                add("dve", lambda e, s=s: e.scalar_tensor_tensor(
                    out=lnB[s][:, :T], in0=lnM[s][:, :T], scalar=-1.0, in1=lnR[s][:, :T], op0=ALU.mult, op1=ALU.mult),
                    reads=[("lnM", s), ("lnR", s)], writes=[("lnB", s)])

            def ln_apply(s, j):
                cw = [("xn", s, 2 * j), ("xn", s, 2 * j + 1)]
                add("dve", lambda e, s=s, j=j: e.tensor_tensor(out=c_view(s, j, T), in0=c_view(s, j, T),
                                                             in1=lnR[s][:, :T], op=ALU.mult),
                    reads=cw + [("lnR", s)], writes=cw)
                add("dve", lambda e, s=s, j=j: e.tensor_tensor(out=c_view(s, j, T), in0=c_view(s, j, T),
                                                             in1=lnB[s][:, :T], op=ALU.add),
                    reads=cw + [("lnB", s)], writes=cw)
                add("act", lambda e, s=s, j=j: e.activation(out=mix[s][:, j, :T], in_=c_view(s, j, T), func=AF.Silu,
                                                          bias=cc(C_LNB + j), scale=cc(C_LNG + j)),
                    reads=cw + [("cst",)], writes=[("mix", s, j)])

            slot = w_get("wpool")
            for g in range(4):
                for s in range(NS):
                    for m in range(2):
                        b = mm6.next()
                        pe_group(b, T, [(ring[slot][:, g * 512 + k * 256 + m * 128:g * 512 + k * 256 + m * 128 + 128],
                                         mix[s][:, 2 * g + k, :T]) for k in range(2)],
                                 [("ring", slot), ("mix", s, 2 * g), ("mix", s, 2 * g + 1)])
                        add("act", lambda e, s=s, g=g, m=m, b=b: e.activation(
                            out=mix[s][:, 8 + 2 * g + m, :T], in_=ps[b][:, :T], func=AF.Copy, scale=cc(C_PSC + 2 * g + m)),
                            reads=[("ps", b), ("cst",)], writes=[("mix", s, 8 + 2 * g + m)])
            pend = []
            for s in range(NS):
                for j in range(8):
                    q = s * 8 + j
                    d = q % 2
                    b = mm4.next()
                    pe_group(b, T, [(diag[d][:, k, :], abf[s][:, j, k:k + T]) for k in range(31)],
                             [("diag", d), ("abf", s, j)])
                    if q + 2 < 16:
                        build_diag(q + 2)
                    cw = [("xn", s, 2 * j), ("xn", s, 2 * j + 1)]
                    add("act", lambda e, s=s, j=j, b=b: e.activation(out=c_view(s, j, T), in_=ps[b][:, :T], func=AF.Identity,
                                                                   bias=cc(C_CONVB + j)),
                        reads=[("ps", b), ("cst",)], writes=cw)
                    t1 = rH.next()
                    add("act", lambda e, j=j, b=b, t1=t1: e.activation(out=tmpH[t1][:, :T], in_=ps[b][:, :T], func=AF.Square,
                                                                     bias=cc(C_CONVB + j)),
                        reads=[("ps", b), ("cst",)], writes=[("tmpH", t1)])
                    t2 = rH.next()
                    add("dve", lambda e, s=s, j=j, t2=t2: e.tensor_copy(out=tmpH[t2][:, :T], in_=c_view(s, j, T)),
                        reads=cw, writes=[("tmpH", t2)])
                    for (ps_, pj, pt1, pt2) in pend:
                        pe_group(4 + 2 * ps_, T, [(ones[:], tmpH[pt2][:, :T])], [("tmpH", pt2), ("ones",)],
                                 start=(pj == 0), stop=(pj == 7))
                        pe_group(5 + 2 * ps_, T, [(ones[:], tmpH[pt1][:, :T])], [("tmpH", pt1), ("ones",)],
                                 start=(pj == 0), stop=(pj == 7))
                    pend = [(s, j, t1, t2)]
                    if s == 1:
                        if j == 0:
                            ln_finalize(0)
                        ln_apply(0, j)
                add("dve", lambda e, s=s: e.tensor_copy(out=abf[s][:, :, 0:HA], in_=abf[s][:, :, T:T + HA]),
                    reads=[("abf", s, j) for j in range(8)], writes=[("abf", s, j) for j in range(8)])
            for (ps_, pj, pt1, pt2) in pend:
                pe_group(4 + 2 * ps_, T, [(ones[:], tmpH[pt2][:, :T])], [("tmpH", pt2), ("ones",)],
                         start=(pj == 0), stop=(pj == 7))
                pe_group(5 + 2 * ps_, T, [(ones[:], tmpH[pt1][:, :T])], [("tmpH", pt1), ("ones",)],
                         start=(pj == 0), stop=(pj == 7))
            ln_finalize(1)
            for j in range(8):
                ln_apply(1, j)
            resid_proj(lambda mp: f"woute{mp}", KC, T, pair=True)

        def ffn(l, T):
            gcol = C_FFN0 if l == 0 else C_FFN1
            for s in range(NS):
                rmsnorm(s, gcol, T, 0)
            xr = lambda s: [("xn", s, k) for k in range(KC)]
            for g in range(NG):
                for jj in range(GJ):
                    slot = w_get(f"wgu{l}_{g * GJ + jj}")
                    for s in range(NS):
                        bg = mm6.next()
                        pe_group(bg, T, [(ring[slot][:, k * 256:k * 256 + 128], xn[s][:, k, :T]) for k in range(KC)],
                                 [("ring", slot)] + xr(s),
                                 fine=([[("ring", slot), ("xn", s, k)] for k in range(KC)] if (g == 0 and jj == 0) else None))
                        tf = rF.next()
                        add("act", lambda e, bg=bg, tf=tf: e.activation(out=tmpF[tf][:, :T], in_=ps[bg][:, :T], func=AF.Silu),
                            reads=[("ps", bg)], writes=[("tmpF", tf)])
                        bu = mm6.next()
                        pe_group(bu, T, [(ring[slot][:, k * 256 + 128:k * 256 + 256], xn[s][:, k, :T]) for k in range(KC)],
                                 [("ring", slot)] + xr(s))
                        add("dve", lambda e, s=s, jj=jj, bu=bu, tf=tf: e.tensor_tensor(
                            out=mix[s][:, jj, :T], in0=ps[bu][:, :T], in1=tmpF[tf][:, :T], op=ALU.mult),
                            reads=[("ps", bu), ("tmpF", tf)], writes=[("mix", s, jj)])
                resid_proj(lambda mp, g=g: f"wdn{l}_{g}_{mp}", GJ, T)

        def l1_mixer(r, T):
            for s in range(NS):
                rmsnorm(s, C_MIXO, T, 0)
            xr = lambda s: [("xn", s, k) for k in range(KC)]
            for j in range(KC):
                s1 = w_get(f"wino_cv{j}")
                s2 = w_get(f"wino_b{j}")
                for s in range(NS):
                    bgc = mm6.next()
                    pe_group(bgc, T, [(ring[s1][:, k * 256:k * 256 + 128], xn[s][:, k, :T]) for k in range(KC)],
                             [("ring", s1)] + xr(s),
                             fine=([[("ring", s1), ("xn", s, k)] for k in range(KC)] if j == 0 else None))
                    tg = rF.next()
                    add("act", lambda e, bgc=bgc, tg=tg: e.activation(out=tmpF[tg][:, :T], in_=ps[bgc][:, :T], func=AF.Copy),
                        reads=[("ps", bgc)], writes=[("tmpF", tg)])
                    bv = mm6.next()
                    pe_group(bv, T, [(ring[s1][:, k * 256 + 128:k * 256 + 256], xn[s][:, k, :T]) for k in range(KC)],
                             [("ring", s1)] + xr(s))
                    cv = rF.next()
                    add("dve", lambda e, bv=bv, tg=tg, cv=cv: e.tensor_tensor(
                        out=tmpF[cv][:, 2:2 + T], in0=ps[bv][:, :T], in1=tmpF[tg][:, :T], op=ALU.mult),
                        reads=[("ps", bv), ("tmpF", tg)], writes=[("tmpF", cv)])
                    add("dve", lambda e, s=s, j=j, cv=cv: e.tensor_copy(out=tmpF[cv][:, 0:2], in_=carry1[s][:, j, :]),
                        reads=[("carry1", s, j), ("tmpF", cv)], writes=[("tmpF", cv)])
                    add("dve", lambda e, j=j, cv=cv, tg=tg: e.tensor_scalar(
                        out=tmpF[tg][:, :T], in0=tmpF[cv][:, 2:2 + T], scalar1=cc(C_CW3 + 3 * j + 2), scalar2=None, op0=ALU.mult),
                        reads=[("tmpF", cv), ("cst",)], writes=[("tmpF", tg)])
                    for k in (1, 0):
                        add("dve", lambda e, j=j, cv=cv, tg=tg, k=k: e.scalar_tensor_tensor(
                            out=tmpF[tg][:, :T], in0=tmpF[cv][:, k:k + T], scalar=cc(C_CW3 + 3 * j + k), in1=tmpF[tg][:, :T],
                            op0=ALU.mult, op1=ALU.add),
                            reads=[("tmpF", cv), ("tmpF", tg), ("cst",)], writes=[("tmpF", tg)])
                    add("dve", lambda e, s=s, j=j, cv=cv: e.tensor_copy(out=carry1[s][:, j, :], in_=tmpF[cv][:, T:T + 2]),
                        reads=[("tmpF", cv)], writes=[("carry1", s, j)])
                    bgb = mm6.next()
                    pe_group(bgb, T, [(ring[s2][:, k * 128:k * 128 + 128], xn[s][:, k, :T]) for k in range(KC)],
                             [("ring", s2)] + xr(s))
                    add("dve", lambda e, s=s, j=j, bgb=bgb, tg=tg: e.tensor_tensor(
                        out=mix[s][:, j, :T], in0=ps[bgb][:, :T], in1=tmpF[tg][:, :T], op=ALU.mult),
                        reads=[("ps", bgb), ("tmpF", tg)], writes=[("mix", s, j)])
            resid_proj(lambda mp: f"wouto{mp}", KC, T)

        add("sp", lambda e: [e.dma_start(out=cst[:], in_=cst_d)], writes=[("cst",)], chan="c")
        add("dve", lambda e: e.memset(ones[:], 1.0), writes=[("ones",)])
        for s in range(NS):
            add("dve", lambda e, s=s: e.memset(abf[s][:], 0.0), writes=[("abf", s, j) for j in range(8)])
            add("dve", lambda e, s=s: e.memset(carryP[s][:], 0.0), writes=[("carryP", s, j) for j in range(8)])
            add("dve", lambda e, s=s: e.memset(carry1[s][:], 0.0), writes=[("carry1", s, j) for j in range(KC)])
        t0 = 0
        for r in range(nrounds):
            cur_round[0] = r
            T = TS[r]
            for s in range(NS):
                add("sp", lambda e, s=s, t0=t0, T=T: [e.dma_start(out=h[s][:, :, :T], in_=x_d[s, :, :, t0:t0 + T])],
                    writes=[("h", s, k) for k in range(KC)], chan=f"x{s}")
            if "l0" in layers:
                l0_mixer(r, T)
            if "f0" in layers:
                ffn(0, T)
            if "l1" in layers:
                l1_mixer(r, T)
            if "f1" in layers:
                ffn(1, T)
            for s in range(NS):
                if final_norm:
                    rmsnorm(s, C_FIN, T, t0, final=True)
                else:
                    raw_store(s, T, t0)
            t0 += T
        add("sp", lambda e: e.nop(), reads=[("ochan", i) for i in range(8)])

        engs = ["pe", "act", "dve", "pool", "sp"]
        eng_sems = {en: [es.enter_context(nc.semaphore(f"s_{en}_{r}")) for r in range(nrounds)] for en in engs}
        chans = [f"w{i}" for i in range(4)] + [f"x{s}" for s in range(NS)] + [f"o{i}" for i in range(8)] + ["c"]
        chan_sems = {c: es.enter_context(nc.semaphore(f"c_{c}")) for c in chans}
        sch.finalize(eng_sems, chan_sems, lambda t: task_round[t.idx])
        block = es.enter_context(nc.Block())

        @block.tensor
        def _(e):
            sch.emit("pe", e)

        @block.scalar
        def _(e):
            sch.emit("act", e)

        @block.vector
        def _(e):
            sch.emit("dve", e)

        @block.gpsimd
        def _(e):
            sch.emit("pool", e)

        @block.sync
        def _(e):
            sch.emit("sp", e)
    return nc


def _blk_cols(W, cols_list):
    out = []
    Wr = W.reshape(KC, P, W.shape[1])
    for cols in cols_list:
        out.append(np.ascontiguousarray(Wr[:, :, cols].transpose(1, 0, 2)).reshape(P, -1))
    return np.stack(out)


def prepare_inputs(inp):
    f = lambda a: np.asarray(a, dtype=np.float32)
    ar = np.arange
    w_in_e = f(inp["w_in_e"])[0]
    cols = []
    for j in range(8):
        cols.append(np.concatenate([1024 + j * 128 + ar(128), j * 128 + ar(128)]))
    for i in range(4):
        cols.append(2048 + i * 256 + ar(256))
    wine = _blk_cols(w_in_e, cols)
    wp = f(inp["w_pool_e"])[0]
    wpool = np.ascontiguousarray(wp.reshape(4, 2, P, 2, 128).transpose(2, 0, 1, 3, 4)).reshape(1, P, 2048)
    c256 = [m * 256 + ar(256) for m in range(8)]
    woute = _blk_cols(f(inp["w_out_e"])[0], c256)
    wouto = _blk_cols(f(inp["w_out_o"])[0], c256)
    wg, wu, wd = f(inp["w_gate"]), f(inp["w_up"]), f(inp["w_down"])
    wgu = []
    for l in range(2):
        cat = np.concatenate([wg[l].reshape(D, NJ, 1, 128), wu[l].reshape(D, NJ, 1, 128)], axis=2)
        wgu.append(np.ascontiguousarray(cat.reshape(KC, P, NJ, 256).transpose(2, 1, 0, 3)).reshape(NJ, P, 4096))
    wgu = np.concatenate(wgu)
    wdn = []
    for l in range(2):
        a = wd[l].reshape(NG, GJ, P, 8, 256).transpose(0, 3, 2, 1, 4)
        wdn.append(np.ascontiguousarray(a).reshape(NG * 8, P, GJ * 256))
    wdn = np.concatenate(wdn)
    w_in_o = f(inp["w_in_o"])[0]
    wino_cv = _blk_cols(w_in_o, [np.concatenate([2048 + j * 128 + ar(128), 4096 + j * 128 + ar(128)]) for j in range(16)])
    wino_b = _blk_cols(w_in_o, [j * 128 + ar(128) for j in range(16)])
    cst = np.zeros((P, NCST), np.float32)
    pk = lambda v: f(v).reshape(-1, P).T
    cst[:, C_MIXE:C_MIXE + 16] = pk(inp["mix_norm_e"][0])
    cst[:, C_FFN0:C_FFN0 + 16] = pk(inp["ffn_norm"][0])
    cst[:, C_MIXO:C_MIXO + 16] = pk(inp["mix_norm_o"][0])
    cst[:, C_FFN1:C_FFN1 + 16] = pk(inp["ffn_norm"][1])
    cst[:, C_FIN:C_FIN + 16] = pk(inp["final_norm"])
    cst[:, C_CONVW:C_CONVW + 248] = f(inp["conv_w_e"])[0].reshape(31, 8, P).transpose(2, 1, 0).reshape(P, 248)
    cst[:, C_CONVB:C_CONVB + 8] = pk(inp["conv_b_e"][0])
    cst[:, C_LNG:C_LNG + 8] = pk(inp["ln_g_e"][0])
    cst[:, C_LNB:C_LNB + 8] = pk(inp["ln_b_e"][0])
    cst[:, C_PSC:C_PSC + 8] = pk(inp["pool_scale_e"][0])
    cst[:, C_CW3:C_CW3 + 48] = f(inp["conv_w_o"])[0].reshape(3, 16, P).transpose(2, 1, 0).reshape(P, 48)
    invc = np.zeros((4, 16), np.float32)
    for g in range(4):
        invc[g] = 1.0 / np.minimum(ar(16) + 1, 2 << g)
    cst[:, C_INVC:C_INVC + 64] = invc.reshape(1, 64)
    cst[:, C_EPS] = EPS
    cst[:, C_ID:C_ID + P] = np.eye(P, dtype=np.float32)
    return dict(cst=cst, wine=wine, wpool=wpool, woute=woute, wgu=wgu, wdn=wdn,
                wino_cv=wino_cv, wino_b=wino_b, wouto=wouto)


def run(inp, nrounds=5, layers=("l0", "f0", "l1", "f1"), final_norm=True, trace=False, ncores=NCORES):
    x = np.asarray(inp["x"], dtype=np.float32)
    shared = prepare_inputs(inp)
    xt = np.ascontiguousarray(x.reshape(16, SEQ, KC, P).transpose(0, 3, 2, 1))
    in_maps = []
    for c in range(ncores):
        m = dict(shared)
        m["x"] = xt[NS * c:NS * c + NS]
        in_maps.append(m)
    nc = build_program(nrounds=nrounds, layers=layers, final_norm=final_norm)
    res = run_bass_kernel_spmd(nc, in_maps, core_ids=list(range(ncores)), trace=trace)
    outs = np.stack([np.asarray(r["out"]) for r in res.results]).reshape(NS * ncores, P, KC, SEQ)
    y = np.ascontiguousarray(outs.transpose(0, 3, 2, 1)).reshape(NS * ncores, SEQ, D)
    return y, res


def kernel(**inputs):
    y, _ = run(inputs)
    return y.astype(np.float32)
```

```python
from contextlib import ExitStack

import numpy as np
import concourse.bass as bass
import concourse.mybir as mybir
from concourse.bass_utils import run_bass_kernel_spmd

F32 = mybir.dt.float32
BF16 = mybir.dt.bfloat16
AF = mybir.ActivationFunctionType
ALU = mybir.AluOpType

P = 128
D = 2048
KC = 16
SEQ = 2048
DFF = 5632
NJ = 44
NG = 4
GJ = 11
NS = 2
TS = [416, 416, 416, 400, 400]
TMAX = 416
HA = 30
HP = 16
EPS = 1e-6
NCORES = 8

C_MIXE = 0
C_FFN0 = 16
C_MIXO = 32
C_FFN1 = 48
C_FIN = 64
C_CONVW = 80
C_CONVB = C_CONVW + 248
C_LNG = C_CONVB + 8
C_LNB = C_LNG + 8
C_PSC = C_LNB + 8
C_CW3 = C_PSC + 8
C_INVC = C_CW3 + 48
C_EPS = C_INVC + 64
C_ID = C_EPS + 1
NCST = C_ID + 128


class _Task:
    __slots__ = ("eng", "fn", "deps", "is_dma", "ndma", "chan", "sig", "needs_sig", "idx")


class Sched:
    def __init__(self):
        self.tasks = []
        self.last_writer = {}
        self.readers = {}
        self.chan_last = {}

    def add(self, eng, fn, reads=(), writes=(), chan=None, ndma=1):
        t = _Task()
        t.eng = eng
        t.fn = fn
        t.is_dma = chan is not None
        t.ndma = ndma
        t.chan = chan
        t.sig = None
        t.needs_sig = chan is not None
        t.idx = len(self.tasks)
        deps = {}

        def dep(o, raw):
            if o is None or o is t:
                return
            if (not o.is_dma) and (not t.is_dma) and o.eng == eng:
                if eng == "pe" or not raw:
                    return
            deps[o.idx] = o

        for r in reads:
            dep(self.last_writer.get(r), True)
        for w in writes:
            dep(self.last_writer.get(w), False)
            for rd in self.readers.get(w, ()):
                dep(rd, False)
        if chan is not None:
            dep(self.chan_last.get(chan), True)
            self.chan_last[chan] = t
        for r in reads:
            self.readers.setdefault(r, []).append(t)
        for w in writes:
            self.last_writer[w] = t
            self.readers[w] = []
        t.deps = list(deps.values())
        for o in t.deps:
            o.needs_sig = True
        self.tasks.append(t)
        return t

    def finalize(self, eng_sems, chan_sems, epoch_of):
        cnt = {}
        for t in self.tasks:
            if not t.needs_sig:
                continue
            if t.is_dma:
                s = chan_sems[t.chan]
                cnt[s.name] = cnt.get(s.name, 0) + 16 * t.ndma
                t.sig = (s, cnt[s.name])
            else:
                s = eng_sems[t.eng][epoch_of(t)]
                cnt[s.name] = cnt.get(s.name, 0) + 1
                t.sig = (s, cnt[s.name])

    def emit(self, eng, e):
        waited = {}
        for t in self.tasks:
            if t.eng != eng:
                continue
            for o in sorted(t.deps, key=lambda o: o.idx):
                s, v = o.sig
                if waited.get(s.name, 0) < v:
                    e.wait_ge(s, v)
                    waited[s.name] = v
            res = t.fn(e)
            if t.needs_sig:
                s, v = t.sig
                if t.is_dma:
                    assert isinstance(res, list) and len(res) == t.ndma
                    for ins in res:
                        ins.then_inc(s, 16)
                else:
                    res.then_inc(s, 1)


class _Rot:
    def __init__(self, items):
        self.items = list(items)
        self.i = 0

    def next(self):
        v = self.items[self.i % len(self.items)]
        self.i += 1
        return v


def build_program(nrounds=5, layers=("l0", "f0", "l1", "f1"), final_norm=True):
    nc = bass.Bass("TRN2", target_bir_lowering=False)
    dt = lambda name, shape: nc.dram_tensor(name, shape, F32, kind="ExternalInput").ap()
    x_d = dt("x", [NS, P, KC, SEQ])
    cst_d = dt("cst", [P, NCST])
    wine_d = dt("wine", [12, P, 4096])
    wpool_d = dt("wpool", [1, P, 2048])
    woute_d = dt("woute", [8, P, 4096])
    wgu_d = dt("wgu", [2 * NJ, P, 4096])
    wdn_d = dt("wdn", [2 * NG * 8, P, GJ * 256])
    wino_cv_d = dt("wino_cv", [16, P, 4096])
    wino_b_d = dt("wino_b", [16, P, 2048])
    wouto_d = dt("wouto", [8, P, 4096])
    out_d = nc.dram_tensor("out", [NS, P, KC, SEQ], F32, kind="ExternalOutput").ap()

    sch = Sched()
    cur_round = [0]
    task_round = {}

    def add(eng, fn, reads=(), writes=(), chan=None, ndma=1):
        t = sch.add(eng, fn, reads, writes, chan, ndma)
        task_round[t.idx] = cur_round[0]
        return t

    with ExitStack() as es:
        sb = lambda name, shape, dty: es.enter_context(nc.sbuf_tensor(name, shape, dty))
        h = [sb(f"h{s}", [P, KC, TMAX], F32) for s in range(NS)]
        xn = [sb(f"xn{s}", [P, KC, TMAX], BF16) for s in range(NS)]
        mix = [sb(f"mix{s}", [P, KC, TMAX], BF16) for s in range(NS)]
        abf = [sb(f"abf{s}", [P, 8, 448], BF16) for s in range(NS)]
        diag = [sb(f"diag{i}", [P, 31, P], BF16) for i in range(2)]
        ring = [sb(f"ring{i}", [P, 4096], BF16) for i in range(4)]
        cst = sb("cst_s", [P, NCST], F32)
        ones = sb("ones", [P, P], BF16)
        rstd = [sb(f"rstd{s}", [P, TMAX], F32) for s in range(NS)]
        lnM = [sb(f"lnM{s}", [P, TMAX], F32) for s in range(NS)]
        lnR = [sb(f"lnR{s}", [P, TMAX], F32) for s in range(NS)]
        lnB = [sb(f"lnB{s}", [P, TMAX], F32) for s in range(NS)]
        carryP = [sb(f"carryP{s}", [P, 8, HP], F32) for s in range(NS)]
        carry1 = [sb(f"carry1{s}", [P, KC, 2], F32) for s in range(NS)]
        fix = sb("fix", [P, 16], F32)
        NTF, NTH = 10, 6
        tmpF = [sb(f"tmpF{i}", [P, HP + TMAX], F32) for i in range(NTF)]
        tmpH = [sb(f"tmpH{i}", [P, TMAX], BF16) for i in range(NTH)]
        ps = [es.enter_context(nc.psum_tensor(f"ps{i}", [P, 512], F32)) for i in range(8)]

        rF = _Rot(range(NTF))
        rH = _Rot(range(NTH))
        mm6 = _Rot(range(6))
        mm4 = _Rot(range(4))
        ochan = _Rot(range(8))

        cc = lambda col: cst[:, col:col + 1]

        def c_view(s, j, T):
            return xn[s][:, 2 * j:2 * j + 2, :].rearrange("p a t -> p (a t)").bitcast(F32)[:, :T]

        blocks = []
        if "l0" in layers:
            blocks += [(f"wine_a{j}", wine_d[j], 4096) for j in range(8)]
            blocks += [(f"wine_b{i}", wine_d[8 + i], 4096) for i in range(4)]
            blocks += [("wpool", wpool_d[0], 2048)]
            blocks += [(f"woute{m}", woute_d[m], 4096) for m in range(8)]

        def ffn_blocks(l):
            out = []
            for g in range(NG):
                out += [(f"wgu{l}_{g * GJ + jj}", wgu_d[l * NJ + g * GJ + jj], 4096) for jj in range(GJ)]
                out += [(f"wdn{l}_{g}_{m}", wdn_d[(l * NG + g) * 8 + m], GJ * 256) for m in range(8)]
            return out

        if "f0" in layers:
            blocks += ffn_blocks(0)
        if "l1" in layers:
            for j in range(16):
                blocks += [(f"wino_cv{j}", wino_cv_d[j], 4096), (f"wino_b{j}", wino_b_d[j], 2048)]
            blocks += [(f"wouto{m}", wouto_d[m], 4096) for m in range(8)]
        if "f1" in layers:
            blocks += ffn_blocks(1)
        wseq = blocks * nrounds
        wstate = {"issued": 0, "consumed": 0}
        NSLOT, AHEAD = 4, 2

        def w_issue(i):
            name, src, nel = wseq[i]
            slot = i % NSLOT
            add("pool", lambda e, slot=slot, src=src, nel=nel: [e.dma_start(out=ring[slot][:, :nel], in_=src)],
                writes=[("ring", slot)], chan=f"w{slot}")

        def w_get(name):
            i = wstate["consumed"]
            assert wseq[i][0] == name, (wseq[i][0], name)
            while wstate["issued"] < min(len(wseq), i + AHEAD + 1):
                w_issue(wstate["issued"])
                wstate["issued"] += 1
            wstate["consumed"] += 1
            return i % NSLOT

        def pe_group(bank, T, pairs, reads, start=True, stop=True, fine=None):
            if fine is not None:
                n = len(pairs)
                for i, (l, r) in enumerate(pairs):
                    add("pe", lambda e, l=l, r=r, i=i, n=n: e.matmul(ps[bank][:, :T], l, r, start=(start and i == 0),
                                                                    stop=(stop and i == n - 1)),
                        reads=fine[i], writes=[("ps", bank)])
                return

            def fn(e, bank=bank, T=T, pairs=pairs, start=start, stop=stop):
                n = len(pairs)
                ins = None
                for i, (l, r) in enumerate(pairs):
                    ins = e.matmul(ps[bank][:, :T], l, r, start=(start and i == 0), stop=(stop and i == n - 1))
                return ins
            add("pe", fn, reads=reads, writes=[("ps", bank)])

        def rmsnorm(s, gcol, T, t0, final=False):
            bank = 6 + s
            for kc in range(KC):
                th = rH.next()
                add("act", lambda e, kc=kc, th=th: e.activation(out=tmpH[th][:, :T], in_=h[s][:, kc, :T], func=AF.Square),
                    reads=[("h", s, kc)], writes=[("tmpH", th)])
                pe_group(bank, T, [(ones[:], tmpH[th][:, :T])], [("tmpH", th), ("ones",)],
                         start=(kc == 0), stop=(kc == KC - 1))
            tf = rF.next()
            add("act", lambda e, tf=tf: e.activation(out=tmpF[tf][:, :T], in_=ps[bank][:, :T], func=AF.Sqrt,
                                                     bias=cc(C_EPS), scale=1.0 / D),
                reads=[("ps", bank), ("cst",)], writes=[("tmpF", tf)])
            add("dve", lambda e, tf=tf: e.reciprocal(out=rstd[s][:, :T], in_=tmpF[tf][:, :T]),
                reads=[("tmpF", tf)], writes=[("rstd", s)])
            for kc in range(KC):
                if not final:
                    add("dve", lambda e, kc=kc: e.scalar_tensor_tensor(
                        out=xn[s][:, kc, :T], in0=h[s][:, kc, :T], scalar=cc(gcol + kc), in1=rstd[s][:, :T],
                        op0=ALU.mult, op1=ALU.mult),
                        reads=[("h", s, kc), ("rstd", s), ("cst",)], writes=[("xn", s, kc)])
                else:
                    tf = rF.next()
                    add("dve", lambda e, kc=kc, tf=tf: e.scalar_tensor_tensor(
                        out=tmpF[tf][:, :T], in0=h[s][:, kc, :T], scalar=cc(gcol + kc), in1=rstd[s][:, :T],
                        op0=ALU.mult, op1=ALU.mult),
                        reads=[("h", s, kc), ("rstd", s), ("cst",)], writes=[("tmpF", tf)])
                    oc = ochan.next()
                    add("sp", lambda e, kc=kc, tf=tf: [e.dma_start(out=out_d[s, :, kc, t0:t0 + T], in_=tmpF[tf][:, :T])],
                        reads=[("tmpF", tf)], writes=[("ochan", oc)], chan=f"o{oc}")

        def raw_store(s, T, t0):
            for kc in range(KC):
                oc = ochan.next()
                add("sp", lambda e, kc=kc: [e.dma_start(out=out_d[s, :, kc, t0:t0 + T], in_=h[s][:, kc, :T])],
                    reads=[("h", s, kc)], writes=[("ochan", oc)], chan=f"o{oc}")

        def resid_proj(wname_fn, nk, T, pair=False):
            def emit(mp, slot, s):
                for c in range(2):
                    m = 2 * mp + c
                    b = mm6.next()
                    pairs = [(ring[slot][:, k * 256 + c * 128:k * 256 + c * 128 + 128], mix[s][:, k, :T])
                             for k in range(nk)]
                    pe_group(b, T, pairs, [("ring", slot)] + [("mix", s, k) for k in range(nk)])
                    add("dve", lambda e, s=s, m=m, b=b: e.tensor_tensor(
                        out=h[s][:, m, :T], in0=h[s][:, m, :T], in1=ps[b][:, :T], op=ALU.add),
                        reads=[("h", s, m), ("ps", b)], writes=[("h", s, m)])

            if pair:
                for mp in range(0, 8, 2):
                    s0 = w_get(wname_fn(mp))
                    s1 = w_get(wname_fn(mp + 1))
                    for s in range(NS):
                        emit(mp, s0, s)
                        emit(mp + 1, s1, s)
            else:
                for mp in range(8):
                    slot = w_get(wname_fn(mp))
                    for s in range(NS):
                        emit(mp, slot, s)

        def l0_mixer(r, T):
            def build_diag(q):
                j, d = q % 8, q % 2

                def fn(e, j=j, d=d):
                    ins = None
                    for k in range(31):
                        ins = e.tensor_scalar(out=diag[d][:, k, :], in0=cst[:, C_ID:C_ID + P],
                                              scalar1=cc(C_CONVW + j * 31 + k), scalar2=None, op0=ALU.mult)
                    return ins
                add("dve", fn, reads=[("cst",)], writes=[("diag", d)])

            build_diag(0)
            build_diag(1)
            for s in range(NS):
                rmsnorm(s, C_MIXE, T, 0)
            xr = lambda s: [("xn", s, k) for k in range(KC)]
            for j in range(8):
                slot = w_get(f"wine_a{j}")
                for s in range(NS):
                    bg = mm6.next()
                    pe_group(bg, T, [(ring[slot][:, k * 256:k * 256 + 128], xn[s][:, k, :T]) for k in range(KC)],
                             [("ring", slot)] + xr(s),
                             fine=([[("ring", slot), ("xn", s, k)] for k in range(KC)] if j == 0 else None))
                    tf = rF.next()
                    add("act", lambda e, bg=bg, tf=tf: e.activation(out=tmpF[tf][:, :T], in_=ps[bg][:, :T], func=AF.Sigmoid),
                        reads=[("ps", bg)], writes=[("tmpF", tf)])
                    bv = mm6.next()
                    pe_group(bv, T, [(ring[slot][:, k * 256 + 128:k * 256 + 256], xn[s][:, k, :T]) for k in range(KC)],
                             [("ring", slot)] + xr(s))
                    add("dve", lambda e, s=s, j=j, bv=bv, tf=tf: e.tensor_tensor(
                        out=abf[s][:, j, HA:HA + T], in0=ps[bv][:, :T], in1=tmpF[tf][:, :T], op=ALU.mult),
                        reads=[("ps", bv), ("tmpF", tf)], writes=[("abf", s, j)])
            for i in range(4):
                slot = w_get(f"wine_b{i}")
                for s in range(NS):
                    for c in range(2):
                        j = 2 * i + c
                        g = j // 2
                        w = 2 << g
                        b = mm6.next()
                        pe_group(b, T, [(ring[slot][:, k * 256 + c * 128:k * 256 + c * 128 + 128], xn[s][:, k, :T])
                                        for k in range(KC)], [("ring", slot)] + xr(s))
                        xb = rF.next()
                        L = HP + T
                        add("act", lambda e, b=b, xb=xb: e.activation(out=tmpF[xb][:, HP:HP + T], in_=ps[b][:, :T], func=AF.Copy),
                            reads=[("ps", b)], writes=[("tmpF", xb)])
                        add("dve", lambda e, s=s, j=j, xb=xb: e.tensor_copy(out=tmpF[xb][:, 0:HP], in_=carryP[s][:, j, :]),
                            reads=[("carryP", s, j), ("tmpF", xb)], writes=[("tmpF", xb)])
                        sa = rF.next()
                        add("dve", lambda e, xb=xb, sa=sa, L=L: e.tensor_tensor(
                            out=tmpF[sa][:, 1:L], in0=tmpF[xb][:, 1:L], in1=tmpF[xb][:, 0:L - 1], op=ALU.add),
                            reads=[("tmpF", xb)], writes=[("tmpF", sa)])
                        cur = sa
                        if g >= 1:
                            sbb = rF.next()
                            add("dve", lambda e, sa=sa, sbb=sbb, L=L: e.tensor_tensor(
                                out=tmpF[sbb][:, 3:L], in0=tmpF[sa][:, 3:L], in1=tmpF[sa][:, 1:L - 2], op=ALU.add),
                                reads=[("tmpF", sa)], writes=[("tmpF", sbb)])
                            cur = sbb
                        if g >= 2:
                            add("dve", lambda e, sa=sa, sbb=sbb, L=L: e.tensor_tensor(
                                out=tmpF[sa][:, 7:L], in0=tmpF[sbb][:, 7:L], in1=tmpF[sbb][:, 3:L - 4], op=ALU.add),
                                reads=[("tmpF", sbb)], writes=[("tmpF", sa)])
                            cur = sa
                        if g >= 3:
                            add("dve", lambda e, sa=sa, sbb=sbb, L=L: e.tensor_tensor(
                                out=tmpF[sbb][:, 15:L], in0=tmpF[sa][:, 15:L], in1=tmpF[sa][:, 7:L - 8], op=ALU.add),
                                reads=[("tmpF", sa)], writes=[("tmpF", sbb)])
                            cur = sbb
                        add("dve", lambda e, s=s, j=j, cur=cur, xb=xb, w=w: e.scalar_tensor_tensor(
                            out=mix[s][:, j, :T], in0=tmpF[cur][:, HP:HP + T], scalar=1.0 / w, in1=tmpF[xb][:, HP:HP + T],
                            op0=ALU.mult, op1=ALU.subtract),
                            reads=[("tmpF", cur), ("tmpF", xb)], writes=[("mix", s, j)])
                        if r == 0:
                            add("dve", lambda e, cur=cur, g=g: e.tensor_tensor(
                                out=fix[:], in0=tmpF[cur][:, HP:HP + 16], in1=cst[:, C_INVC + 16 * g:C_INVC + 16 * g + 16],
                                op=ALU.mult),
                                reads=[("tmpF", cur), ("cst",)], writes=[("fix",)])
                            add("dve", lambda e, s=s, j=j, xb=xb: e.tensor_tensor(
                                out=mix[s][:, j, 0:16], in0=fix[:], in1=tmpF[xb][:, HP:HP + 16], op=ALU.subtract),
                                reads=[("fix",), ("tmpF", xb), ("mix", s, j)], writes=[("mix", s, j)])
                        add("dve", lambda e, s=s, j=j, xb=xb: e.tensor_copy(out=carryP[s][:, j, :], in_=tmpF[xb][:, T:T + HP]),
                            reads=[("tmpF", xb)], writes=[("carryP", s, j)])

            def ln_finalize(s):
                b1, b2 = 4 + 2 * s, 5 + 2 * s
                t1 = rF.next()
                add("dve", lambda e, s=s, b1=b1: e.tensor_scalar(out=lnM[s][:, :T], in0=ps[b1][:, :T], scalar1=1.0 / 1024,
                                                               scalar2=None, op0=ALU.mult),
                    reads=[("ps", b1)], writes=[("lnM", s)])
                add("dve", lambda e, s=s, t1=t1: e.tensor_tensor(out=tmpF[t1][:, :T], in0=lnM[s][:, :T], in1=lnM[s][:, :T],
                                                               op=ALU.mult),
                    reads=[("lnM", s)], writes=[("tmpF", t1)])
                add("dve", lambda e, s=s, t1=t1, b2=b2: e.scalar_tensor_tensor(
                    out=tmpF[t1][:, :T], in0=ps[b2][:, :T], scalar=1.0 / 1024, in1=tmpF[t1][:, :T],
                    op0=ALU.mult, op1=ALU.subtract),
                    reads=[("ps", b2), ("tmpF", t1)], writes=[("tmpF", t1)])
                add("act", lambda e, t1=t1: e.activation(out=tmpF[t1][:, :T], in_=tmpF[t1][:, :T], func=AF.Sqrt,
                                                         bias=cc(C_EPS), scale=1.0),
                    reads=[("tmpF", t1), ("cst",)], writes=[("tmpF", t1)])
                add("dve", lambda e, s=s, t1=t1: e.reciprocal(out=lnR[s][:, :T], in_=tmpF[t1][:, :T]),
                    reads=[("tmpF", t1)], writes=[("lnR", s)])
                add("dve", lambda e, s=s: e.scalar_tensor_tensor(
                    out=lnB[s][:, :T], in0=lnM[s][:, :T], scalar=-1.0, in1=lnR[s][:, :T], op0=ALU.mult, op1=ALU.mult),
                    reads=[("lnM", s), ("lnR", s)], writes=[("lnB", s)])

            def ln_apply(s, j):
                cw = [("xn", s, 2 * j), ("xn", s, 2 * j + 1)]
                add("dve", lambda e, s=s, j=j: e.tensor_tensor(out=c_view(s, j, T), in0=c_view(s, j, T),
                                                             in1=lnR[s][:, :T], op=ALU.mult),
                    reads=cw + [("lnR", s)], writes=cw)
                add("dve", lambda e, s=s, j=j: e.tensor_tensor(out=c_view(s, j, T), in0=c_view(s, j, T),
                                                             in1=lnB[s][:, :T], op=ALU.add),
                    reads=cw + [("lnB", s)], writes=cw)
                add("act", lambda e, s=s, j=j: e.activation(out=mix[s][:, j, :T], in_=c_view(s, j, T), func=AF.Silu,
                                                          bias=cc(C_LNB + j), scale=cc(C_LNG + j)),
                    reads=cw + [("cst",)], writes=[("mix", s, j)])

            slot = w_get("wpool")
            for g in range(4):
                for s in range(NS):
                    for m in range(2):
                        b = mm6.next()
                        pe_group(b, T, [(ring[slot][:, g * 512 + k * 256 + m * 128:g * 512 + k * 256 + m * 128 + 128],
                                         mix[s][:, 2 * g + k, :T]) for k in range(2)],
                                 [("ring", slot), ("mix", s, 2 * g), ("mix", s, 2 * g + 1)])
                        add("act", lambda e, s=s, g=g, m=m, b=b: e.activation(
                            out=mix[s][:, 8 + 2 * g + m, :T], in_=ps[b][:, :T], func=AF.Copy, scale=cc(C_PSC + 2 * g + m)),
                            reads=[("ps", b), ("cst",)], writes=[("mix", s, 8 + 2 * g + m)])
            pend = []
            for s in range(NS):
                for j in range(8):
                    q = s * 8 + j
                    d = q % 2
                    b = mm4.next()
                    pe_group(b, T, [(diag[d][:, k, :], abf[s][:, j, k:k + T]) for k in range(31)],
                             [("diag", d), ("abf", s, j)])
                    if q + 2 < 16:
                        build_diag(q + 2)
                    cw = [("xn", s, 2 * j), ("xn", s, 2 * j + 1)]
                    add("act", lambda e, s=s, j=j, b=b: e.activation(out=c_view(s, j, T), in_=ps[b][:, :T], func=AF.Identity,
                                                                   bias=cc(C_CONVB + j)),
                        reads=[("ps", b), ("cst",)], writes=cw)
                    t1 = rH.next()
                    add("act", lambda e, j=j, b=b, t1=t1: e.activation(out=tmpH[t1][:, :T], in_=ps[b][:, :T], func=AF.Square,
                                                                     bias=cc(C_CONVB + j)),
                        reads=[("ps", b), ("cst",)], writes=[("tmpH", t1)])
                    t2 = rH.next()
                    add("dve", lambda e, s=s, j=j, t2=t2: e.tensor_copy(out=tmpH[t2][:, :T], in_=c_view(s, j, T)),
                        reads=cw, writes=[("tmpH", t2)])
                    for (ps_, pj, pt1, pt2) in pend:
                        pe_group(4 + 2 * ps_, T, [(ones[:], tmpH[pt2][:, :T])], [("tmpH", pt2), ("ones",)],
                                 start=(pj == 0), stop=(pj == 7))
                        pe_group(5 + 2 * ps_, T, [(ones[:], tmpH[pt1][:, :T])], [("tmpH", pt1), ("ones",)],
                                 start=(pj == 0), stop=(pj == 7))
                    pend = [(s, j, t1, t2)]
                    if s == 1:
                        if j == 0:
                            ln_finalize(0)
                        ln_apply(0, j)
                add("dve", lambda e, s=s: e.tensor_copy(out=abf[s][:, :, 0:HA], in_=abf[s][:, :, T:T + HA]),
                    reads=[("abf", s, j) for j in range(8)], writes=[("abf", s, j) for j in range(8)])
            for (ps_, pj, pt1, pt2) in pend:
                pe_group(4 + 2 * ps_, T, [(ones[:], tmpH[pt2][:, :T])], [("tmpH", pt2), ("ones",)],
                         start=(pj == 0), stop=(pj == 7))
                pe_group(5 + 2 * ps_, T, [(ones[:], tmpH[pt1][:, :T])], [("tmpH", pt1), ("ones",)],
                         start=(pj == 0), stop=(pj == 7))
            ln_finalize(1)
            for j in range(8):
                ln_apply(1, j)
            resid_proj(lambda mp: f"woute{mp}", KC, T, pair=True)

        def ffn(l, T):
            gcol = C_FFN0 if l == 0 else C_FFN1
            for s in range(NS):
                rmsnorm(s, gcol, T, 0)
            xr = lambda s: [("xn", s, k) for k in range(KC)]
            for g in range(NG):
                for jj in range(GJ):
                    slot = w_get(f"wgu{l}_{g * GJ + jj}")
                    for s in range(NS):
                        bg = mm6.next()
                        pe_group(bg, T, [(ring[slot][:, k * 256:k * 256 + 128], xn[s][:, k, :T]) for k in range(KC)],
                                 [("ring", slot)] + xr(s),
                                 fine=([[("ring", slot), ("xn", s, k)] for k in range(KC)] if (g == 0 and jj == 0) else None))
                        tf = rF.next()
                        add("act", lambda e, bg=bg, tf=tf: e.activation(out=tmpF[tf][:, :T], in_=ps[bg][:, :T], func=AF.Silu),
                            reads=[("ps", bg)], writes=[("tmpF", tf)])
                        bu = mm6.next()
                        pe_group(bu, T, [(ring[slot][:, k * 256 + 128:k * 256 + 256], xn[s][:, k, :T]) for k in range(KC)],
                                 [("ring", slot)] + xr(s))
                        add("dve", lambda e, s=s, jj=jj, bu=bu, tf=tf: e.tensor_tensor(
                            out=mix[s][:, jj, :T], in0=ps[bu][:, :T], in1=tmpF[tf][:, :T], op=ALU.mult),
                            reads=[("ps", bu), ("tmpF", tf)], writes=[("mix", s, jj)])
                resid_proj(lambda mp, g=g: f"wdn{l}_{g}_{mp}", GJ, T)

        def l1_mixer(r, T):
            for s in range(NS):
                rmsnorm(s, C_MIXO, T, 0)
            xr = lambda s: [("xn", s, k) for k in range(KC)]
            for j in range(KC):
                s1 = w_get(f"wino_cv{j}")
                s2 = w_get(f"wino_b{j}")
                for s in range(NS):
                    bgc = mm6.next()
                    pe_group(bgc, T, [(ring[s1][:, k * 256:k * 256 + 128], xn[s][:, k, :T]) for k in range(KC)],
                             [("ring", s1)] + xr(s),
                             fine=([[("ring", s1), ("xn", s, k)] for k in range(KC)] if j == 0 else None))
                    tg = rF.next()
                    add("act", lambda e, bgc=bgc, tg=tg: e.activation(out=tmpF[tg][:, :T], in_=ps[bgc][:, :T], func=AF.Copy),
                        reads=[("ps", bgc)], writes=[("tmpF", tg)])
                    bv = mm6.next()
                    pe_group(bv, T, [(ring[s1][:, k * 256 + 128:k * 256 + 256], xn[s][:, k, :T]) for k in range(KC)],
                             [("ring", s1)] + xr(s))
                    cv = rF.next()
                    add("dve", lambda e, bv=bv, tg=tg, cv=cv: e.tensor_tensor(
                        out=tmpF[cv][:, 2:2 + T], in0=ps[bv][:, :T], in1=tmpF[tg][:, :T], op=ALU.mult),
                        reads=[("ps", bv), ("tmpF", tg)], writes=[("tmpF", cv)])
                    add("dve", lambda e, s=s, j=j, cv=cv: e.tensor_copy(out=tmpF[cv][:, 0:2], in_=carry1[s][:, j, :]),
                        reads=[("carry1", s, j), ("tmpF", cv)], writes=[("tmpF", cv)])
                    add("dve", lambda e, j=j, cv=cv, tg=tg: e.tensor_scalar(
                        out=tmpF[tg][:, :T], in0=tmpF[cv][:, 2:2 + T], scalar1=cc(C_CW3 + 3 * j + 2), scalar2=None, op0=ALU.mult),
                        reads=[("tmpF", cv), ("cst",)], writes=[("tmpF", tg)])
                    for k in (1, 0):
                        add("dve", lambda e, j=j, cv=cv, tg=tg, k=k: e.scalar_tensor_tensor(
                            out=tmpF[tg][:, :T], in0=tmpF[cv][:, k:k + T], scalar=cc(C_CW3 + 3 * j + k), in1=tmpF[tg][:, :T],
                            op0=ALU.mult, op1=ALU.add),
                            reads=[("tmpF", cv), ("tmpF", tg), ("cst",)], writes=[("tmpF", tg)])
                    add("dve", lambda e, s=s, j=j, cv=cv: e.tensor_copy(out=carry1[s][:, j, :], in_=tmpF[cv][:, T:T + 2]),
                        reads=[("tmpF", cv)], writes=[("carry1", s, j)])
                    bgb = mm6.next()
                    pe_group(bgb, T, [(ring[s2][:, k * 128:k * 128 + 128], xn[s][:, k, :T]) for k in range(KC)],
                             [("ring", s2)] + xr(s))
                    add("dve", lambda e, s=s, j=j, bgb=bgb, tg=tg: e.tensor_tensor(
                        out=mix[s][:, j, :T], in0=ps[bgb][:, :T], in1=tmpF[tg][:, :T], op=ALU.mult),
                        reads=[("ps", bgb), ("tmpF", tg)], writes=[("mix", s, j)])
            resid_proj(lambda mp: f"wouto{mp}", KC, T)

        add("sp", lambda e: [e.dma_start(out=cst[:], in_=cst_d)], writes=[("cst",)], chan="c")
        add("dve", lambda e: e.memset(ones[:], 1.0), writes=[("ones",)])
        for s in range(NS):
            add("dve", lambda e, s=s: e.memset(abf[s][:], 0.0), writes=[("abf", s, j) for j in range(8)])
            add("dve", lambda e, s=s: e.memset(carryP[s][:], 0.0), writes=[("carryP", s, j) for j in range(8)])
            add("dve", lambda e, s=s: e.memset(carry1[s][:], 0.0), writes=[("carry1", s, j) for j in range(KC)])
        t0 = 0
        for r in range(nrounds):
            cur_round[0] = r
            T = TS[r]
            for s in range(NS):
                add("sp", lambda e, s=s, t0=t0, T=T: [e.dma_start(out=h[s][:, :, :T], in_=x_d[s, :, :, t0:t0 + T])],
                    writes=[("h", s, k) for k in range(KC)], chan=f"x{s}")
            if "l0" in layers:
                l0_mixer(r, T)
            if "f0" in layers:
                ffn(0, T)
            if "l1" in layers:
                l1_mixer(r, T)
            if "f1" in layers:
                ffn(1, T)
            for s in range(NS):
                if final_norm:
                    rmsnorm(s, C_FIN, T, t0, final=True)
                else:
                    raw_store(s, T, t0)
            t0 += T
        add("sp", lambda e: e.nop(), reads=[("ochan", i) for i in range(8)])

        engs = ["pe", "act", "dve", "pool", "sp"]
        eng_sems = {en: [es.enter_context(nc.semaphore(f"s_{en}_{r}")) for r in range(nrounds)] for en in engs}
        chans = [f"w{i}" for i in range(4)] + [f"x{s}" for s in range(NS)] + [f"o{i}" for i in range(8)] + ["c"]
        chan_sems = {c: es.enter_context(nc.semaphore(f"c_{c}")) for c in chans}
        sch.finalize(eng_sems, chan_sems, lambda t: task_round[t.idx])
        block = es.enter_context(nc.Block())

        @block.tensor
        def _(e):
            sch.emit("pe", e)

        @block.scalar
        def _(e):
            sch.emit("act", e)

        @block.vector
        def _(e):
            sch.emit("dve", e)

        @block.gpsimd
        def _(e):
            sch.emit("pool", e)

        @block.sync
        def _(e):
            sch.emit("sp", e)
    return nc


def _blk_cols(W, cols_list):
    out = []
    Wr = W.reshape(KC, P, W.shape[1])
    for cols in cols_list:
        out.append(np.ascontiguousarray(Wr[:, :, cols].transpose(1, 0, 2)).reshape(P, -1))
    return np.stack(out)


def prepare_inputs(inp):
    f = lambda a: np.asarray(a, dtype=np.float32)
    ar = np.arange
    w_in_e = f(inp["w_in_e"])[0]
    cols = []
    for j in range(8):
        cols.append(np.concatenate([1024 + j * 128 + ar(128), j * 128 + ar(128)]))
    for i in range(4):
        cols.append(2048 + i * 256 + ar(256))
    wine = _blk_cols(w_in_e, cols)
    wp = f(inp["w_pool_e"])[0]
    wpool = np.ascontiguousarray(wp.reshape(4, 2, P, 2, 128).transpose(2, 0, 1, 3, 4)).reshape(1, P, 2048)
    c256 = [m * 256 + ar(256) for m in range(8)]
    woute = _blk_cols(f(inp["w_out_e"])[0], c256)
    wouto = _blk_cols(f(inp["w_out_o"])[0], c256)
    wg, wu, wd = f(inp["w_gate"]), f(inp["w_up"]), f(inp["w_down"])
    wgu = []
    for l in range(2):
        cat = np.concatenate([wg[l].reshape(D, NJ, 1, 128), wu[l].reshape(D, NJ, 1, 128)], axis=2)
        wgu.append(np.ascontiguousarray(cat.reshape(KC, P, NJ, 256).transpose(2, 1, 0, 3)).reshape(NJ, P, 4096))
    wgu = np.concatenate(wgu)
    wdn = []
    for l in range(2):
        a = wd[l].reshape(NG, GJ, P, 8, 256).transpose(0, 3, 2, 1, 4)
        wdn.append(np.ascontiguousarray(a).reshape(NG * 8, P, GJ * 256))
    wdn = np.concatenate(wdn)
    w_in_o = f(inp["w_in_o"])[0]
    wino_cv = _blk_cols(w_in_o, [np.concatenate([2048 + j * 128 + ar(128), 4096 + j * 128 + ar(128)]) for j in range(16)])
    wino_b = _blk_cols(w_in_o, [j * 128 + ar(128) for j in range(16)])
    cst = np.zeros((P, NCST), np.float32)
    pk = lambda v: f(v).reshape(-1, P).T
    cst[:, C_MIXE:C_MIXE + 16] = pk(inp["mix_norm_e"][0])
    cst[:, C_FFN0:C_FFN0 + 16] = pk(inp["ffn_norm"][0])
    cst[:, C_MIXO:C_MIXO + 16] = pk(inp["mix_norm_o"][0])
    cst[:, C_FFN1:C_FFN1 + 16] = pk(inp["ffn_norm"][1])
    cst[:, C_FIN:C_FIN + 16] = pk(inp["final_norm"])
    cst[:, C_CONVW:C_CONVW + 248] = f(inp["conv_w_e"])[0].reshape(31, 8, P).transpose(2, 1, 0).reshape(P, 248)
    cst[:, C_CONVB:C_CONVB + 8] = pk(inp["conv_b_e"][0])
    cst[:, C_LNG:C_LNG + 8] = pk(inp["ln_g_e"][0])
    cst[:, C_LNB:C_LNB + 8] = pk(inp["ln_b_e"][0])
    cst[:, C_PSC:C_PSC + 8] = pk(inp["pool_scale_e"][0])
    cst[:, C_CW3:C_CW3 + 48] = f(inp["conv_w_o"])[0].reshape(3, 16, P).transpose(2, 1, 0).reshape(P, 48)
    invc = np.zeros((4, 16), np.float32)
    for g in range(4):
        invc[g] = 1.0 / np.minimum(ar(16) + 1, 2 << g)
    cst[:, C_INVC:C_INVC + 64] = invc.reshape(1, 64)
    cst[:, C_EPS] = EPS
    cst[:, C_ID:C_ID + P] = np.eye(P, dtype=np.float32)
    return dict(cst=cst, wine=wine, wpool=wpool, woute=woute, wgu=wgu, wdn=wdn,
                wino_cv=wino_cv, wino_b=wino_b, wouto=wouto)


def run(inp, nrounds=5, layers=("l0", "f0", "l1", "f1"), final_norm=True, trace=False, ncores=NCORES):
    x = np.asarray(inp["x"], dtype=np.float32)
    shared = prepare_inputs(inp)
    xt = np.ascontiguousarray(x.reshape(16, SEQ, KC, P).transpose(0, 3, 2, 1))
    in_maps = []
    for c in range(ncores):
        m = dict(shared)
        m["x"] = xt[NS * c:NS * c + NS]
        in_maps.append(m)
    nc = build_program(nrounds=nrounds, layers=layers, final_norm=final_norm)
    res = run_bass_kernel_spmd(nc, in_maps, core_ids=list(range(ncores)), trace=trace)
    outs = np.stack([np.asarray(r["out"]) for r in res.results]).reshape(NS * ncores, P, KC, SEQ)
    y = np.ascontiguousarray(outs.transpose(0, 3, 2, 1)).reshape(NS * ncores, SEQ, D)
    return y, res


def kernel(**inputs):
    y, _ = run(inputs)
    return y.astype(np.float32)
```
